# Optimizing a Trainium2 kernel written in Bass

```python
import jax, jax.numpy as jnp
from jax import lax
import numpy as np

D_MODEL = 2048
BATCH = 2
SEQ = 4096
DEPTH = 1

N_META = 16
POOL_WIDTH = D_MODEL
POOL_WINDOWS = (2, 4, 8, 16)
POOL_GROUPS = len(POOL_WINDOWS)
POOL_GROUP_DIM = POOL_WIDTH // POOL_GROUPS
LRU_WIDTH = D_MODEL
LRU_HEAD_DIM = 256
LRU_HEADS = LRU_WIDTH // LRU_HEAD_DIM
CONV_WIDTH = 4
LRU_C = 8.0
D_FF = 4 * D_MODEL
NORM_EPS = 1e-6
IN_SPLITS = (POOL_WIDTH,
             POOL_WIDTH + LRU_WIDTH,
             POOL_WIDTH + 2 * LRU_WIDTH,
             POOL_WIDTH + 2 * LRU_WIDTH + D_MODEL)
IN_COLS = POOL_WIDTH + 2 * LRU_WIDTH + 2 * D_MODEL

kernel_name = "hybrid_pool_rglru_gated_block"


def rmsnorm(x, g):
    xf = x.astype(jnp.float32)
    y = xf * lax.rsqrt(jnp.mean(xf * xf, axis=-1, keepdims=True) + NORM_EPS)
    return (y * g.astype(jnp.float32)).astype(x.dtype)


def causal_window_mean(v, w):
    T = v.shape[1]
    c = lax.cumsum(v, axis=1)
    c_shift = jnp.pad(c, ((0, 0), (w, 0), (0, 0)))[:, :T]
    cnt = jnp.minimum(jnp.arange(1, T + 1), w).astype(jnp.float32)
    return (c - c_shift) / cnt[None, :, None]


def pool_mixer(v, pool_w, pool_scale):
    B, T, _ = v.shape
    vf = v.astype(jnp.float32)
    diffs = []
    for g, w in enumerate(POOL_WINDOWS):
        vg = vf[..., g * POOL_GROUP_DIM:(g + 1) * POOL_GROUP_DIM]
        diffs.append(causal_window_mean(vg, w) - vg)
    d = jnp.stack(diffs, axis=2)
    y = jnp.einsum('btgc,gcd->btgd', d, pool_w.astype(jnp.float32))
    y = y.reshape(B, T, POOL_WIDTH) * pool_scale.astype(jnp.float32)
    return y.astype(v.dtype)


def causal_depthwise_conv(x, w, b):
    y = lax.conv_general_dilated(
        x, w[:, None, :].astype(x.dtype), window_strides=(1,),
        padding=((CONV_WIDTH - 1, 0),),
        dimension_numbers=('NWC', 'WIO', 'NWC'),
        feature_group_count=x.shape[-1])
    return y + b.astype(x.dtype)


def rg_lru(xc, gate_a_w, gate_a_b, gate_x_w, gate_x_b, lam):
    B, T, W = xc.shape
    xf = xc.astype(jnp.float32)
    xh = xf.reshape(B, T, LRU_HEADS, LRU_HEAD_DIM)
    r = jax.nn.sigmoid(jnp.einsum('bthi,hij->bthj', xh, gate_a_w.astype(jnp.float32))
                       + gate_a_b.astype(jnp.float32)).reshape(B, T, W)
    i = jax.nn.sigmoid(jnp.einsum('bthi,hij->bthj', xh, gate_x_w.astype(jnp.float32))
                       + gate_x_b.astype(jnp.float32)).reshape(B, T, W)
    log_a = -LRU_C * r * jax.nn.softplus(-lam.astype(jnp.float32))
    a = jnp.exp(log_a)
    mult = jnp.sqrt(-jnp.expm1(2.0 * log_a))
    bt = mult * (i * xf)

    def combine(left, right):
        a1, b1 = left
        a2, b2 = right
        return a1 * a2, a2 * b1 + b2

    _, h = lax.associative_scan(combine, (a, bt), axis=1)
    return h.astype(xc.dtype)


def setup_inputs(seed: int = 0) -> dict:
    key = jax.random.key(seed)
    ks = jax.random.split(key, 20)
    f32 = jnp.float32
    x = jax.random.normal(ks[0], (BATCH, SEQ, D_MODEL), f32)
    meta_tokens = jax.random.normal(ks[1], (N_META, D_MODEL), f32)
    norm1_g = 1.0 + 0.02 * jax.random.normal(ks[2], (DEPTH, D_MODEL), f32)
    w_in = jax.random.normal(ks[3], (DEPTH, D_MODEL, IN_COLS), f32) * D_MODEL ** -0.5
    pool_w = jax.random.normal(ks[4], (DEPTH, POOL_GROUPS, POOL_GROUP_DIM, POOL_GROUP_DIM), f32) * POOL_GROUP_DIM ** -0.5
    pool_scale = 1.0 + 0.02 * jax.random.normal(ks[5], (DEPTH, POOL_WIDTH), f32)
    conv_w = jax.random.normal(ks[6], (DEPTH, CONV_WIDTH, LRU_WIDTH), f32) * CONV_WIDTH ** -0.5
    conv_b = 0.01 * jax.random.normal(ks[7], (DEPTH, LRU_WIDTH), f32)
    gate_a_w = jax.random.normal(ks[8], (DEPTH, LRU_HEADS, LRU_HEAD_DIM, LRU_HEAD_DIM), f32) * LRU_HEAD_DIM ** -0.5
    gate_a_b = 0.01 * jax.random.normal(ks[9], (DEPTH, LRU_HEADS, LRU_HEAD_DIM), f32)
    gate_x_w = jax.random.normal(ks[10], (DEPTH, LRU_HEADS, LRU_HEAD_DIM, LRU_HEAD_DIM), f32) * LRU_HEAD_DIM ** -0.5
    gate_x_b = 0.01 * jax.random.normal(ks[11], (DEPTH, LRU_HEADS, LRU_HEAD_DIM), f32)
    u = jax.random.uniform(ks[12], (DEPTH, LRU_WIDTH), f32, minval=0.9, maxval=0.999)
    s = u ** (1.0 / LRU_C)
    lru_lambda = jnp.log(s) - jnp.log1p(-s)
    w_out = jax.random.normal(ks[13], (DEPTH, D_MODEL, D_MODEL), f32) * D_MODEL ** -0.5
    norm2_g = 1.0 + 0.02 * jax.random.normal(ks[14], (DEPTH, D_MODEL), f32)
    mlp_w1 = jax.random.normal(ks[15], (DEPTH, D_MODEL, D_FF), f32) * D_MODEL ** -0.5
    mlp_w2 = jax.random.normal(ks[16], (DEPTH, D_FF, D_MODEL), f32) * D_FF ** -0.5
    final_g = 1.0 + 0.02 * jax.random.normal(ks[17], (D_MODEL,), f32)
    return {"x": x, "meta_tokens": meta_tokens, "norm1_g": norm1_g, "w_in": w_in,
            "pool_w": pool_w, "pool_scale": pool_scale, "conv_w": conv_w, "conv_b": conv_b,
            "gate_a_w": gate_a_w, "gate_a_b": gate_a_b, "gate_x_w": gate_x_w, "gate_x_b": gate_x_b,
            "lru_lambda": lru_lambda, "w_out": w_out, "norm2_g": norm2_g,
            "mlp_w1": mlp_w1, "mlp_w2": mlp_w2, "final_g": final_g}


def reference(x, meta_tokens, norm1_g, w_in, pool_w, pool_scale, conv_w, conv_b,
              gate_a_w, gate_a_b, gate_x_w, gate_x_b, lru_lambda, w_out, norm2_g,
              mlp_w1, mlp_w2, final_g):
    B = x.shape[0]
    meta = jnp.broadcast_to(meta_tokens[None].astype(x.dtype), (B, N_META, x.shape[-1]))
    h = jnp.concatenate([meta, x], axis=1)
    for l in range(DEPTH):
        u = rmsnorm(h, norm1_g[l])
        proj = u @ w_in[l]
        v_pool, v_lru, v_gelu, g_pool, g_lru = jnp.split(proj, IN_SPLITS, axis=-1)
        pool_out = pool_mixer(v_pool, pool_w[l], pool_scale[l])
        xc = causal_depthwise_conv(v_lru, conv_w[l], conv_b[l])
        lru_out = rg_lru(xc, gate_a_w[l], gate_a_b[l], gate_x_w[l], gate_x_b[l],
                         lru_lambda[l]) * jax.nn.gelu(v_gelu)
        merged = jax.nn.sigmoid(g_pool) * pool_out + jax.nn.sigmoid(g_lru) * lru_out
        h = h + merged @ w_out[l]
        u2 = rmsnorm(h, norm2_g[l])
        h = h + jnp.square(jax.nn.relu(u2 @ mlp_w1[l])) @ mlp_w2[l]
    out = rmsnorm(h, final_g)
    return out[:, N_META:]
```

```python
import numpy as np
import concourse.bass as bass
import concourse.mybir as mybir
from concourse.bass_utils import run_bass_kernel_spmd

F32 = mybir.dt.float32
BF16 = mybir.dt.bfloat16
AF = mybir.ActivationFunctionType
ALU = mybir.AluOpType

D = 2048
NCH = 16
HALO = 16
TOK = 1024
NT = TOK + HALO
NCORES = 8
EPS = 1e-6
SEGS = [(0, HALO), (HALO, HALO + 512), (HALO + 512, NT)]
OSEGS = [(0, 512), (512, 1024)]
NSLOT = 12


class Tracker:
    def __init__(self, nc):
        self.nc = nc
        self.eng = {"pe": nc.tensor, "act": nc.scalar, "dve": nc.vector, "pool": nc.gpsimd, "sp": nc.sync}
        self.sems = {}
        self.val = {}
        for n in self.eng:
            self.sems["e_" + n] = nc.alloc_semaphore("sem_" + n)
            self.val["e_" + n] = 0
        self.dq = {"sp": [], "pool": []}
        self.dqn = {"sp": 0, "pool": 0}
        for q in self.dq:
            for k in range(8):
                key = "d_%s%d" % (q, k)
                self.sems[key] = nc.alloc_semaphore("sem_" + key)
                self.val[key] = 0
                self.dq[q].append(key)
        self.waited = {n: {} for n in self.eng}
        self.lastw = {}
        self.reads = {}

    def _deps(self, reads, writes):
        d = {}

        def add(ev):
            if ev is not None and d.get(ev[0], 0) < ev[1]:
                d[ev[0]] = ev[1]
        for r in reads:
            add(self.lastw.get(r))
        for w in writes:
            add(self.lastw.get(w))
            for ev in self.reads.get(w, ()):
                add(ev)
        return d

    def _wait(self, en, d):
        e = self.eng[en]
        wt = self.waited[en]
        for k, v in d.items():
            if en == "pe" and k == "e_pe":
                continue
            if wt.get(k, 0) < v:
                e.wait_ge(self.sems[k], v)
                wt[k] = v

    def _commit(self, ev, reads, writes):
        for w in writes:
            self.lastw[w] = ev
            self.reads[w] = []
        for r in reads:
            self.reads.setdefault(r, []).append(ev)

    def op(self, en, fn, reads=(), writes=()):
        self._wait(en, self._deps(reads, writes))
        ins = fn(self.eng[en])
        k = "e_" + en
        self.val[k] += 1
        ins.then_inc(self.sems[k], 1)
        self._commit((k, self.val[k]), reads, writes)

    def dma(self, q, pairs, reads=(), writes=()):
        d = self._deps(reads, writes)
        k = self.dq[q][self.dqn[q] % len(self.dq[q])]
        self.dqn[q] += 1
        if self.val[k] > 0 and d.get(k, 0) < self.val[k]:
            d[k] = self.val[k]
        self._wait(q, d)
        e = self.eng[q]
        for (o, i) in pairs:
            e.dma_start(out=o, in_=i).then_inc(self.sems[k], 16)
            self.val[k] += 16
        self._commit((k, self.val[k]), reads, writes)

    def barrier(self):
        for en in self.eng:
            self._wait(en, dict(self.val))
        self.lastw = {}
        self.reads = {}

    def final_wait(self, en):
        self._wait(en, dict(self.val))


def build_program():
    nc = bass.Bass("TRN2", target_bir_lowering=False)
    T = Tracker(nc)

    xin = nc.dram_tensor("xin", [NT, D], F32, kind="ExternalInput").ap()
    w_in = nc.dram_tensor("w_in", [D, 5 * D], F32, kind="ExternalInput").ap()
    pool_w = nc.dram_tensor("pool_w", [4, 512, 512], F32, kind="ExternalInput").ap()
    gate_a_w = nc.dram_tensor("gate_a_w", [8, 256, 256], F32, kind="ExternalInput").ap()
    gate_x_w = nc.dram_tensor("gate_x_w", [8, 256, 256], F32, kind="ExternalInput").ap()
    w_out = nc.dram_tensor("w_out", [D, D], F32, kind="ExternalInput").ap()
    w1 = nc.dram_tensor("mlp_w1", [D, 4 * D], F32, kind="ExternalInput").ap()
    w2 = nc.dram_tensor("mlp_w2", [4 * D, D], F32, kind="ExternalInput").ap()
    pc_d = nc.dram_tensor("pc", [128, 9, NCH], F32, kind="ExternalInput").ap()
    gbc_d = nc.dram_tensor("gbc3", [3, 128, D], F32, kind="ExternalInput").ap()
    flag_d = nc.dram_tensor("flag", [128, 1], F32, kind="ExternalInput").ap()
    mask_d = nc.dram_tensor("mask3", [128, 8, 4], F32, kind="ExternalInput").ap()
    eye_d = nc.dram_tensor("eye", [128, 128], F32, kind="ExternalInput").ap()
    out_d = nc.dram_tensor("out", [TOK, D], F32, kind="ExternalOutput").ap()
    cc_in = [nc.dram_tensor("cc_in%d" % g, [128, 8], F32) for g in range(4)]
    cc_out = [nc.dram_tensor("cc_out%d" % g, [128 * NCORES, 8], F32) for g in range(4)]

    mT = nc.alloc_sbuf_tensor("mT", [128, NCH, TOK], BF16)
    ring = nc.alloc_sbuf_tensor("ring", [128, NSLOT * 2048], BF16)
    ARENA = 29696
    arena = nc.alloc_sbuf_tensor("arena", [128, ARENA], F32)
    pcs = nc.alloc_sbuf_tensor("pcs", [128, 9, NCH], F32)
    cvec = nc.alloc_sbuf_tensor("cvec", [128, NCH], F32)
    c2vec = nc.alloc_sbuf_tensor("c2vec", [128, NCH], F32)
    eyef = nc.alloc_sbuf_tensor("eyef", [128, 128], F32)
    ident = nc.alloc_sbuf_tensor("ident", [128, 128], BF16)
    flag = nc.alloc_sbuf_tensor("flagt", [128, 1], F32)
    ones = nc.alloc_sbuf_tensor("ones", [128, 1], F32)
    epsc = nc.alloc_sbuf_tensor("epsc", [128, 1], F32)
    mask3 = nc.alloc_sbuf_tensor("mask3t", [128, 8, 4], F32)
    ssq = nc.alloc_sbuf_tensor("ssq", [128, 32], F32)
    rstd = nc.alloc_sbuf_tensor("rstd", [128, 32], F32)
    sm = nc.alloc_sbuf_tensor("sm", [128, 4, 8], F32)
    gath = nc.alloc_sbuf_tensor("gath", [128, 4, 8, 8], F32)
    pmk = nc.alloc_sbuf_tensor("pmk", [128, 8, 4], F32)
    hmk = nc.alloc_sbuf_tensor("hmk", [128, 8, 4], F32)
    hs = nc.alloc_sbuf_tensor("hs", [128, 4, 8], F32)
    rs = nc.alloc_sbuf_tensor("rs", [128, NCH, 2], F32)
    rsum = nc.alloc_sbuf_tensor("rsum", [128, NCH], F32)
    psb = [nc.alloc_psum_tensor("ps%d" % b, [128, 512], F32) for b in range(8)]

    def carve(off, n, dt=F32):
        a = arena[:, off:off + n]
        return a.bitcast(BF16) if dt == BF16 else a

    uT = carve(0, 8320, BF16).rearrange("p (c t) -> p c t", c=NCH)
    WK = 8320
    xt = [carve(WK + 2048 * k, 2048) for k in range(2)]
    ubA = [carve(WK + 4096 + 1024 * k, 1024, BF16) for k in range(2)]
    gbcA = carve(WK + 6144, 2048)
    o = WK
    a_st = [carve(o + 1040 * i, 1040) for i in range(4)]; o += 4160
    b_st = [carve(o + 1040 * i, 1040) for i in range(4)]; o += 4160
    vl = [carve(o + 1044 * i, 1044) for i in range(2)]; o += 2088
    xc = [carve(o + 1040 * i, 1040) for i in range(2)]; o += 2080
    xcb = [carve(o + 520 * i, 520, BF16) for i in range(2)]; o += 1040
    tmp = [carve(o + 1040 * i, 1040) for i in range(2)]; o += 2080
    vp = carve(o, 1040); o += 1040
    sA = carve(o, 1040); o += 1040
    sB = carve(o, 1040); o += 1040
    diffb = carve(o, 2048, BF16).rearrange("p (c t) -> p c t", c=4); o += 2048
    sgp, SGP = sA[:, 0:TOK], ("sA",)
    ge, GE = sB[:, 0:TOK], ("sB",)
    sgl, SGL = vp[:, 0:TOK], ("vp",)
    assert o <= ARENA, o
    hB = carve(0, 16384).rearrange("p (t d) -> p t d", t=8)
    h1T = carve(16384, 8192, BF16).rearrange("p (c t) -> p c t", c=NCH)
    ubB = [carve(24576 + 1024 * k, 1024, BF16) for k in range(2)]
    gbcB = carve(26624, 2048)
    rtmp = [carve(28672 + 512 * k, 512) for k in range(2)]
    assert 28672 + 1024 <= ARENA

    bank_ctr = [0]

    def nbank():
        b = bank_ctr[0] % 8
        bank_ctr[0] += 1
        return b

    def PE(fn, r, w): T.op("pe", fn, r, w)
    def ACT(fn, r, w): T.op("act", fn, r, w)
    def DVE(fn, r, w): T.op("dve", fn, r, w)

    def mm_group(b, n, pairs, reads):
        def fn(e):
            ins = None
            for i, (l, r) in enumerate(pairs):
                ins = e.matmul(psb[b][:, 0:n], l, r, start=(i == 0), stop=(i == len(pairs) - 1))
            return ins
        PE(fn, reads, [("ps", b)])

    class WStream:
        def __init__(self):
            self.blocks = []
            self.next_emit = 0
            self.ptr = 0
            self.owner = [None] * NSLOT

        def add(self, m, mk_pairs):
            p = self.ptr % NSLOT
            if m == 4 and p % 4 != 0:
                self.ptr += 4 - p % 4
                p = self.ptr % NSLOT
            assert p + m <= NSLOT
            self.blocks.append(dict(m=m, mk=mk_pairs, pos=p, emitted=False, done=False))
            self.ptr += m
            return len(self.blocks) - 1

        def view(self, j):
            b = self.blocks[j]
            return ring[:, b["pos"] * 2048:(b["pos"] + b["m"]) * 2048]

        def keys(self, j):
            b = self.blocks[j]
            return [("w", s) for s in range(b["pos"], b["pos"] + b["m"])]

        def pump(self):
            while self.next_emit < len(self.blocks):
                j = self.next_emit
                b = self.blocks[j]
                for s in range(b["pos"], b["pos"] + b["m"]):
                    ow = self.owner[s]
                    if ow is not None and not self.blocks[ow]["done"]:
                        return
                for s in range(b["pos"], b["pos"] + b["m"]):
                    self.owner[s] = j
                T.dma("pool", b["mk"](self.view(j)), reads=[], writes=self.keys(j))
                b["emitted"] = True
                self.next_emit += 1

        def get(self, j):
            self.pump()
            assert self.blocks[j]["emitted"], ("weight ring too small for schedule", j)
            return self.view(j), self.keys(j)

        def done(self, j):
            self.blocks[j]["done"] = True
            self.pump()

    W = WStream()

    def kview(dram2d):
        return dram2d.rearrange("(k p) c -> p k c", p=128)

    def std_block(dram2d, ncols):
        K = 16 if dram2d.shape[0] == 2048 else dram2d.shape[0] // 128
        m = (K * ncols * 2 + 4095) // 4096

        def mk(view):
            return [(view[:, 0:K * ncols].rearrange("p (k c) -> p k c", k=K), kview(dram2d))]
        return W.add(m, mk), K

    PROJ = {"pool": 0, "lru": 1, "gelu": 2, "gpool": 3, "glru": 4}
    wb = {}

    def reg_win(p, c):
        wb[(p, c)] = std_block(w_in[:, PROJ[p] * D + c * 128: PROJ[p] * D + (c + 1) * 128], 128)[0]

    for g in range(4):
        c0 = 4 * g
        reg_win("lru", c0); reg_win("lru", c0 + 1); reg_win("pool", c0); reg_win("pool", c0 + 1)

        def mk_gates(view, g=g):
            v = view.rearrange("p (a h k o) -> p a h k o", a=2, h=2, k=2)
            prs = []
            for a_i, gw in enumerate((gate_a_w, gate_x_w)):
                for hl in range(2):
                    prs.append((v[:, a_i, hl], kview(gw[2 * g + hl])))
            return prs
        wb[("gates", g)] = W.add(1, mk_gates)
        reg_win("lru", c0 + 2); reg_win("lru", c0 + 3); reg_win("pool", c0 + 2); reg_win("pool", c0 + 3)
        wb[("poolw", g)] = std_block(pool_w[g], 512)[0]
        for lc in range(4):
            reg_win("gpool", c0 + lc); reg_win("gelu", c0 + lc); reg_win("glru", c0 + lc)
    for cb in range(4):
        wb[("wout", cb)] = std_block(w_out[:, cb * 512:(cb + 1) * 512], 512)[0]
    for q in range(4):
        for fb in range(4):
            wb[("w1", q, fb)] = std_block(w1[:, q * 2048 + fb * 512: q * 2048 + (fb + 1) * 512], 512)[0]
        for cb in range(4):
            wb[("w2", q, cb)] = std_block(w2[q * 2048:(q + 1) * 2048, cb * 512:(cb + 1) * 512], 512)[0]

    T.dma("sp", [(pcs[:], pc_d)], [], [("pc",)])
    T.dma("sp", [(flag[:], flag_d)], [], [("flag",)])
    T.dma("sp", [(mask3[:], mask_d)], [], [("mask",)])
    T.dma("sp", [(eyef[:], eye_d)], [], [("eyef",)])
    T.dma("sp", [(gbcA, gbc_d[0])], [], [("gbc",)])
    DVE(lambda e: e.memset(epsc[:], EPS), [], [("epsc",)])
    DVE(lambda e: e.memset(ones[:], 1.0), [("epsc",)], [("ones",)])
    DVE(lambda e: e.memset(ssq[:], 0.0), [], [("ssq",)])
    DVE(lambda e: e.memset(rs[:], 0.0), [], [("rs",)])
    DVE(lambda e: e.tensor_copy(out=ident[:], in_=eyef[:]), [("eyef",)], [("ident",)])
    ACT(lambda e: e.activation(out=cvec[:], in_=pcs[:, 8, :], func=AF.Exp, scale=-1.0), [("pc",)], [("cvec",)])
    ACT(lambda e: e.activation(out=cvec[:], in_=cvec[:], func=AF.Ln, bias=ones[:, 0:1]), [("cvec",), ("ones",)], [("cvec",)])
    DVE(lambda e: e.tensor_single_scalar(out=c2vec[:], in_=cvec[:], scalar=-16.0, op=ALU.mult), [("cvec",)], [("c2vec",)])
    DVE(lambda e: e.tensor_single_scalar(out=cvec[:], in_=cvec[:], scalar=-8.0, op=ALU.mult), [("cvec",), ("c2vec",)], [("cvec",)])
    W.pump()

    ncol = [0]

    def norm_tile(src_ap, src_keys, nrows, ub, ubkey, gb, dstT, col0, dst_keys):
        j = ncol[0]; ncol[0] += 1
        ACT(lambda e: e.activation(out=ub[0:nrows, :], in_=src_ap, func=AF.Square, accum_out=ssq[0:nrows, j:j + 1]),
            src_keys + [("ssq",)], [ubkey, ("ssqc", j)])
        ACT(lambda e: e.activation(out=rstd[0:nrows, j:j + 1], in_=ssq[0:nrows, j:j + 1], func=AF.Sqrt, scale=1.0 / D,
                                   bias=epsc[0:nrows, 0:1]), [("ssqc", j), ("ones",)], [("rstd", j)])
        DVE(lambda e: e.reciprocal(out=rstd[0:nrows, j:j + 1], in_=rstd[0:nrows, j:j + 1]), [("rstd", j)], [("rstd", j)])
        DVE(lambda e: e.scalar_tensor_tensor(out=ub[0:nrows, :], in0=src_ap, scalar=rstd[0:nrows, j:j + 1], in1=gb[0:nrows, :],
                                             op0=ALU.mult, op1=ALU.mult),
            src_keys + [("rstd", j), ("gbc",)], [ubkey])
        for half in range(2):
            b = nbank()
            pv = psb[b][:, :].bitcast(BF16)

            def fn(e, half=half, pv=pv):
                ins = None
                for cc in range(8):
                    c = half * 8 + cc
                    ins = e.transpose(pv[:, cc * nrows:(cc + 1) * nrows], ub[0:nrows, c * 128:(c + 1) * 128],
                                      ident[0:nrows, 0:nrows])
                return ins
            PE(fn, [ubkey, ("ident",)], [("ps", b)])
            src = pv[:, 0:8 * nrows].rearrange("p (c t) -> p c t", c=8)
            dst = dstT[:, half * 8:(half + 1) * 8, col0:col0 + nrows]
            if half == 0:
                ACT(lambda e, src=src, dst=dst: e.activation(out=dst, in_=src, func=AF.Identity), [], [("ps", b)] + dst_keys[half])
            else:
                DVE(lambda e, src=src, dst=dst: e.tensor_copy(out=dst, in_=src), [], [("ps", b)] + dst_keys[half])

    for ti in range(9):
        k = ti % 2
        nrows = HALO if ti == 0 else 128
        r0 = 0 if ti == 0 else HALO + (ti - 1) * 128
        T.dma("sp", [(xt[k][0:nrows, :], xin[r0:r0 + nrows, :])], [], [("xt", k)])
        norm_tile(xt[k][0:nrows, :], [("xt", k)], nrows, ubA[k], ("ub", k), gbcA, uT, r0, [[("uT", ti)], [("uT", ti)]])

    T.barrier()

    DVE(lambda e: e.memset(vl[0][:, 0:3], 0.0), [], [("vl", 0)])
    DVE(lambda e: e.memset(vl[1][:, 0:3], 0.0), [], [("vl", 1)])

    def seg_uT_keys(lo, hi):
        ks = []
        for ti in range(9):
            a0 = 0 if ti == 0 else HALO + (ti - 1) * 128
            a1 = HALO if ti == 0 else a0 + 128
            if a0 < hi and a1 > lo:
                ks.append(("uT", ti))
        return ks

    def mT_keys(c, lo=0, hi=TOK):
        return [("mT", c, ts) for ts in range(lo // 128, (hi + 127) // 128)]

    def proj(p, c, segs, evac):
        j = wb[(p, c)]
        view, wkeys = W.get(j)
        wv = view.rearrange("p (k c) -> p k c", k=16)
        for (lo, hi) in segs:
            b = nbank()
            n = hi - lo
            mm_group(b, n, [(wv[:, k, :], uT[:, k, lo:hi]) for k in range(16)], wkeys + seg_uT_keys(lo, hi))
            evac(b, n, lo, hi)
        W.done(j)

    for g in range(4):
        nsteps = g + 1
        win = 2 ** (g + 1)
        gview = [None]
        gkeys = [None]
        for head in range(2):
            lcs = (2 * head, 2 * head + 1)
            for lc in lcs:
                s = lc % 2
                c = 4 * g + lc

                def ev_l(b, n, lo, hi, s=s):
                    ACT(lambda e: e.activation(out=vl[s][:, 3 + lo:3 + hi], in_=psb[b][:, 0:n], func=AF.Identity),
                        [], [("ps", b), ("vl", s)])
                proj("lru", c, SEGS, ev_l)
                DVE(lambda e, s=s, c=c: e.tensor_scalar(out=xc[s], in0=vl[s][:, 3:3 + NT], scalar1=pcs[:, 4, c:c + 1],
                                                        scalar2=pcs[:, 5, c:c + 1], op0=ALU.mult, op1=ALU.add),
                    [("vl", s), ("pc",)], [("xc", s)])
                for kk in range(3):
                    DVE(lambda e, s=s, c=c, kk=kk: e.scalar_tensor_tensor(out=xc[s], in0=vl[s][:, kk:kk + NT],
                                                                          scalar=pcs[:, 1 + kk, c:c + 1], in1=xc[s],
                                                                          op0=ALU.mult, op1=ALU.add),
                        [("vl", s), ("pc",)], [("xc", s)])
                DVE(lambda e, s=s: e.tensor_copy(out=xcb[s], in_=xc[s]), [("xc", s)], [("xcb", s)])
            for lc in lcs:
                c = 4 * g + lc

                def ev_p(b, n, lo, hi):
                    ACT(lambda e: e.activation(out=vp[:, lo:hi], in_=psb[b][:, 0:n], func=AF.Identity),
                        [], [("ps", b), ("vp",)])
                proj("pool", c, SEGS, ev_p)
                src, skey = vp, ("vp",)
                for st in range(nsteps):
                    sh = 2 ** st
                    lo_ = 2 ** (st + 1) - 1
                    dst, dkey = (sA, ("sA",)) if st % 2 == 0 else (sB, ("sB",))
                    DVE(lambda e, src=src, dst=dst, sh=sh, lo_=lo_: e.tensor_tensor(
                        out=dst[:, lo_:NT], in0=src[:, lo_:NT], in1=src[:, lo_ - sh:NT - sh], op=ALU.add),
                        [skey], [dkey])
                    src, skey = dst, dkey
                DVE(lambda e, src=src, lc=lc: e.scalar_tensor_tensor(out=diffb[:, lc, :], in0=src[:, HALO:NT], scalar=1.0 / win,
                                                                     in1=vp[:, HALO:NT], op0=ALU.mult, op1=ALU.subtract),
                    [skey, ("vp",)], [("diffb", lc)])
            if head == 0:
                gview[0], gkeys[0] = W.get(wb[("gates", g)])
            gv = gview[0].rearrange("p (a h k o) -> p a h k o", a=2, h=2, k=2)
            for a_i in range(2):
                for jj in range(2):
                    lc = lcs[jj]
                    s = lc % 2
                    c = 4 * g + lc
                    for si, (lo, hi) in enumerate(SEGS):
                        b = nbank()
                        n = hi - lo
                        mm_group(b, n, [(gv[:, a_i, head, kc, jj * 128:(jj + 1) * 128], xcb[kc][:, lo:hi]) for kc in range(2)],
                                 gkeys[0] + [("xcb", 0), ("xcb", 1)])
                        if a_i == 0:
                            if si == 0:
                                ACT(lambda e, b=b, n=n, lo=lo, hi=hi, s=s, c=c: e.activation(
                                    out=vl[s][:, 3 + lo:3 + hi], in_=psb[b][:, 0:n], func=AF.Sigmoid, bias=pcs[:, 6, c:c + 1]),
                                    [("pc",), ("xc", s)], [("ps", b), ("vl", s)])
                            else:
                                ACT(lambda e, b=b, n=n, lo=lo, hi=hi, s=s, c=c, si=si: e.activation(
                                    out=vl[s][:, 3 + lo:3 + hi], in_=psb[b][:, 0:n], func=AF.Sigmoid, bias=pcs[:, 6, c:c + 1],
                                    accum_out=rs[:, c, si - 1:si]),
                                    [("pc",), ("rs",), ("xc", s)], [("ps", b), ("vl", s), ("rsc", c, si)])
                        else:
                            ACT(lambda e, b=b, n=n, lo=lo, hi=hi, lc=lc, c=c: e.activation(
                                out=b_st[lc][:, lo:hi], in_=psb[b][:, 0:n], func=AF.Sigmoid, bias=pcs[:, 7, c:c + 1]),
                                [("pc",)], [("ps", b), ("b", lc)])
            if head == 1:
                W.done(wb[("gates", g)])
            for lc in lcs:
                s = lc % 2
                c = 4 * g + lc
                r_ap = vl[s][:, 3:3 + NT]
                ACT(lambda e, lc=lc, c=c, r_ap=r_ap: e.activation(out=a_st[lc], in_=r_ap, func=AF.Exp, scale=cvec[:, c:c + 1]),
                    [("vl", s), ("cvec",)], [("a", lc)])
                ACT(lambda e, s=s, c=c, r_ap=r_ap: e.activation(out=tmp[s], in_=r_ap, func=AF.Exp, scale=c2vec[:, c:c + 1]),
                    [("vl", s), ("c2vec",)], [("tmp", s)])
                DVE(lambda e, s=s: e.tensor_single_scalar(out=tmp[s], in_=tmp[s], scalar=1.0, op=ALU.min), [], [("tmp", s)])
                ACT(lambda e, s=s: e.activation(out=tmp[s], in_=tmp[s], func=AF.Sqrt, scale=-1.0, bias=ones[:, 0:1]),
                    [("ones",)], [("tmp", s)])
                DVE(lambda e, s=s, lc=lc: e.tensor_tensor(out=b_st[lc], in0=b_st[lc], in1=xc[s], op=ALU.mult),
                    [("xc", s)], [("b", lc)])
                DVE(lambda e, s=s, lc=lc: e.tensor_tensor(out=b_st[lc], in0=b_st[lc], in1=tmp[s], op=ALU.mult),
                    [("tmp", s)], [("b", lc)])
                DVE(lambda e, lc=lc: e.tensor_scalar(out=a_st[lc][:, 0:HALO], in0=a_st[lc][:, 0:HALO], scalar1=-1.0,
                                                     scalar2=flag[:, 0:1], op0=ALU.add, op1=ALU.mult),
                    [("flag",)], [("a", lc)])
                DVE(lambda e, lc=lc: e.tensor_single_scalar(out=a_st[lc][:, 0:HALO], in_=a_st[lc][:, 0:HALO], scalar=1.0, op=ALU.add),
                    [], [("a", lc)])
                DVE(lambda e, lc=lc: e.tensor_single_scalar(out=b_st[lc][:, 0:HALO], in_=b_st[lc][:, 0:HALO], scalar=flag[:, 0:1],
                                                            op=ALU.mult), [("flag",)], [("b", lc)])
                DVE(lambda e, s=s, lc=lc: e.tensor_tensor_scan(out=tmp[s], data0=a_st[lc], data1=b_st[lc], initial=0.0,
                                                               op0=ALU.mult, op1=ALU.add),
                    [("a", lc), ("b", lc)], [("tmp", s)])
                DVE(lambda e, s=s, lc=lc: e.tensor_copy(out=sm[:, g, lc:lc + 1], in_=tmp[s][:, NT - 1:NT]),
                    [("tmp", s)], [("sm", g, lc)])
                DVE(lambda e, c=c: e.tensor_tensor(out=rsum[:, c:c + 1], in0=rs[:, c, 0:1], in1=rs[:, c, 1:2], op=ALU.add),
                    [("rsc", c, 1), ("rsc", c, 2)], [("rsum", c)])
        for lc in range(4):
            c = 4 * g + lc
            ACT(lambda e, lc=lc, c=c: e.activation(out=sm[:, g, 4 + lc:5 + lc], in_=rsum[:, c:c + 1], func=AF.Exp,
                                                   scale=cvec[:, c:c + 1]), [("rsum", c), ("cvec",)], [("sm", g, 4 + lc)])
        smk = [("sm", g, i) for i in range(8)]
        T.dma("sp", [(cc_in[g].ap(), sm[:, g, :])], smk, [("ccin", g)])
        T.op("pool", lambda e: e.collective_compute("AllGather", ALU.bypass, replica_groups=[list(range(NCORES))],
                                                    ins=[cc_in[g].ap().opt()], outs=[cc_out[g].ap().opt()]),
             [("ccin", g)], [("ccout", g)])
        T.dma("sp", [(gath[:, g], cc_out[g].ap().rearrange("(r p) f -> p r f", p=128))], [("ccout", g)], [("gath", g)])
        DVE(lambda e: e.tensor_single_scalar(out=pmk[:], in_=gath[:, g, :, 4:8], scalar=-1.0, op=ALU.add), [("gath", g)], [("pmk",)])
        DVE(lambda e: e.tensor_tensor(out=pmk[:], in0=pmk[:], in1=mask3[:], op=ALU.mult), [("mask",)], [("pmk",)])
        DVE(lambda e: e.tensor_single_scalar(out=pmk[:], in_=pmk[:], scalar=1.0, op=ALU.add), [], [("pmk",)])
        DVE(lambda e: e.tensor_tensor(out=hmk[:], in0=gath[:, g, :, 0:4], in1=mask3[:], op=ALU.mult), [("gath", g), ("mask",)], [("hmk",)])
        for lc in range(4):
            DVE(lambda e, lc=lc: e.tensor_tensor_scan(out=hs[:, lc, :], data0=pmk[:, :, lc], data1=hmk[:, :, lc], initial=0.0,
                                                      op0=ALU.mult, op1=ALU.add), [("pmk",), ("hmk",)], [("hs", lc)])
        pwv, pwkeys = W.get(wb[("poolw", g)])
        pwv = pwv.rearrange("p (k c) -> p k c", k=4)
        for lc in range(4):
            s = lc % 2
            c = 4 * g + lc
            ybanks = []
            for (lo, hi) in OSEGS:
                b = nbank()
                rk = pwkeys + [("diffb", kk) for kk in range(4)]
                mm_group(b, 512, [(pwv[:, kk, lc * 128:(lc + 1) * 128], diffb[:, kk, lo:hi]) for kk in range(4)], rk)
                ybanks.append(b)
            if lc == 3:
                W.done(wb[("poolw", g)])

            def ev_gp(b, n, lo, hi):
                ACT(lambda e: e.activation(out=sgp[:, lo - HALO:hi - HALO], in_=psb[b][:, 0:n], func=AF.Sigmoid),
                    [], [("ps", b), SGP])
            proj("gpool", c, SEGS[1:], ev_gp)
            for (lo, hi), b in zip(OSEGS, ybanks):
                DVE(lambda e, lo=lo, hi=hi, b=b, c=c: e.scalar_tensor_tensor(
                    out=sgp[:, lo:hi], in0=psb[b][:, 0:512], scalar=pcs[:, 0, c:c + 1], in1=sgp[:, lo:hi],
                    op0=ALU.mult, op1=ALU.mult), [("pc",)], [("ps", b), SGP])

            def ev_ge(b, n, lo, hi):
                ACT(lambda e: e.activation(out=ge[:, lo - HALO:hi - HALO], in_=psb[b][:, 0:n], func=AF.Gelu_apprx_tanh),
                    [], [("ps", b), GE])
            proj("gelu", c, SEGS[1:], ev_ge)

            def ev_gl(b, n, lo, hi):
                ACT(lambda e: e.activation(out=sgl[:, lo - HALO:hi - HALO], in_=psb[b][:, 0:n], func=AF.Sigmoid),
                    [], [("ps", b), SGL])
            proj("glru", c, SEGS[1:], ev_gl)
            DVE(lambda e: e.tensor_tensor(out=ge, in0=ge, in1=sgl, op=ALU.mult), [SGL], [GE])
            DVE(lambda e, s=s, lc=lc: e.tensor_tensor_scan(out=tmp[s], data0=a_st[lc], data1=b_st[lc], initial=hs[:, lc, 7:8],
                                                           op0=ALU.mult, op1=ALU.add),
                [("a", lc), ("b", lc), ("hs", lc)], [("tmp", s)])
            DVE(lambda e, s=s: e.tensor_tensor(out=ge, in0=ge, in1=tmp[s][:, HALO:NT], op=ALU.mult), [("tmp", s)], [GE])
            DVE(lambda e, c=c: e.tensor_tensor(out=mT[:, c, :], in0=sgp, in1=ge, op=ALU.add), [SGP, GE], mT_keys(c))

    T.barrier()

    T.dma("sp", [(gbcB, gbc_d[1])], [], [("gbc",)])
    for ts in range(8):
        T.dma("sp", [(hB[:, ts, :], xin[HALO + ts * 128:HALO + (ts + 1) * 128, :])], [], [("h", ts, cb) for cb in range(4)])

    def tok_major_mm(wkey, actT, act_keys_fn, cb):
        j = wb[wkey]
        view, wkeys = W.get(j)
        wv = view.rearrange("p (k c) -> p k c", k=16)
        for ts in range(8):
            b = nbank()
            mm_group(b, 512, [(actT[:, k, ts * 128:(ts + 1) * 128], wv[:, k, :]) for k in range(16)], wkeys + act_keys_fn(ts))
            DVE(lambda e, ts=ts, b=b: e.tensor_tensor(out=hB[:, ts, cb * 512:(cb + 1) * 512], in0=hB[:, ts, cb * 512:(cb + 1) * 512],
                                                      in1=psb[b][:, 0:512], op=ALU.add), [], [("ps", b), ("h", ts, cb)])
        W.done(j)

    for cb in range(4):
        tok_major_mm(("wout", cb), mT, lambda ts: [("mT", c, ts) for c in range(16)], cb)

    for ts in range(8):
        k = ts % 2
        hk = [("h", ts, cb) for cb in range(4)]
        norm_tile(hB[:, ts, :], hk, 128, ubB[k], ("ub", k), gbcB, mT, ts * 128,
                  [[("mT", c, ts) for c in range(8)], [("mT", c, ts) for c in range(8, 16)]])
    T.dma("sp", [(gbcB, gbc_d[2])], [], [("gbc",)])

    for q in range(4):
        for fb in range(4):
            j = wb[("w1", q, fb)]
            view, wkeys = W.get(j)
            wv = view.rearrange("p (k c) -> p k c", k=16)
            for fc in range(4):
                fl = fb * 4 + fc
                for tt in range(2):
                    b = nbank()
                    rk = wkeys + [("mT", c, ts) for c in range(16) for ts in range(tt * 4, tt * 4 + 4)]
                    mm_group(b, 512, [(wv[:, k, fc * 128:(fc + 1) * 128], mT[:, k, tt * 512:(tt + 1) * 512]) for k in range(16)], rk)
                    kk = bank_ctr[0] % 2
                    ACT(lambda e, b=b, kk=kk: e.activation(out=rtmp[kk], in_=psb[b][:, 0:512], func=AF.Relu),
                        [], [("ps", b), ("rtmp", kk)])
                    DVE(lambda e, kk=kk, fl=fl, tt=tt: e.tensor_tensor(out=h1T[:, fl, tt * 512:(tt + 1) * 512], in0=rtmp[kk],
                                                                      in1=rtmp[kk], op=ALU.mult),
                        [], [("rtmp", kk), ("h1T", fl, tt)])
            W.done(j)
        for cb in range(4):
            tok_major_mm(("w2", q, cb), h1T, lambda ts: [("h1T", fl, ts // 4) for fl in range(16)], cb)

    for ts in range(8):
        j = ncol[0]; ncol[0] += 1
        k = ts % 2
        hk = [("h", ts, cb) for cb in range(4)]
        ACT(lambda e, ts=ts, j=j, k=k: e.activation(out=ubB[k], in_=hB[:, ts, :], func=AF.Square, accum_out=ssq[:, j:j + 1]),
            hk + [("ssq",)], [("ub", k), ("ssqc", j)])
        ACT(lambda e, j=j: e.activation(out=rstd[:, j:j + 1], in_=ssq[:, j:j + 1], func=AF.Sqrt, scale=1.0 / D,
                                        bias=epsc[:, 0:1]), [("ssqc", j), ("ones",)], [("rstd", j)])
        DVE(lambda e, j=j: e.reciprocal(out=rstd[:, j:j + 1], in_=rstd[:, j:j + 1]), [("rstd", j)], [("rstd", j)])
        DVE(lambda e, ts=ts, j=j: e.scalar_tensor_tensor(out=hB[:, ts, :], in0=hB[:, ts, :], scalar=rstd[:, j:j + 1], in1=gbcB,
                                                         op0=ALU.mult, op1=ALU.mult), [("rstd", j), ("gbc",)], hk)
        T.dma("sp", [(out_d[ts * 128:(ts + 1) * 128, :], hB[:, ts, :])], hk, [("out", ts)])
    T.final_wait("sp")
    return nc


_CACHE = {}


def kernel(x, meta_tokens, norm1_g, w_in, pool_w, pool_scale, conv_w, conv_b,
           gate_a_w, gate_a_b, gate_x_w, gate_x_b, lru_lambda, w_out, norm2_g,
           mlp_w1, mlp_w2, final_g):
    f = lambda a: np.ascontiguousarray(np.asarray(a, dtype=np.float32))
    x = f(x); meta = f(meta_tokens)

    def chan(v):
        return f(v).reshape(NCH, 128).T
    cw = f(conv_w)[0]
    pc = np.stack([chan(f(pool_scale)[0]), chan(cw[0]), chan(cw[1]), chan(cw[2]), chan(cw[3]), chan(f(conv_b)[0]),
                   chan(f(gate_a_b)[0].reshape(-1)), chan(f(gate_x_b)[0].reshape(-1)), chan(f(lru_lambda)[0])], axis=1)
    pc = np.ascontiguousarray(pc, dtype=np.float32)
    gbc3 = np.ascontiguousarray(np.stack([np.broadcast_to(f(norm1_g)[0], (128, D)),
                                          np.broadcast_to(f(norm2_g)[0], (128, D)),
                                          np.broadcast_to(f(final_g), (128, D))]), dtype=np.float32)
    shared = {"w_in": f(w_in)[0], "pool_w": f(pool_w)[0], "gate_a_w": f(gate_a_w)[0], "gate_x_w": f(gate_x_w)[0],
              "w_out": f(w_out)[0], "mlp_w1": f(mlp_w1)[0], "mlp_w2": f(mlp_w2)[0], "pc": pc, "gbc3": gbc3,
              "eye": np.eye(128, dtype=np.float32)}
    in_maps = []
    for k in range(NCORES):
        b, j = k // 4, k % 4
        s0 = j * TOK
        halo = meta if j == 0 else x[b, s0 - HALO:s0]
        m = dict(shared)
        m["xin"] = np.ascontiguousarray(np.concatenate([halo, x[b, s0:s0 + TOK]], axis=0))
        m["flag"] = np.full((128, 1), 1.0 if j == 0 else 0.0, np.float32)
        mk = np.zeros((128, 8, 4), np.float32)
        for r in range(NCORES):
            if r // 4 == b and r < k:
                mk[:, r, :] = 1.0
        m["mask3"] = mk
        in_maps.append(m)
    if "nc" not in _CACHE:
        _CACHE["nc"] = build_program()
    res = run_bass_kernel_spmd(_CACHE["nc"], in_maps, core_ids=list(range(NCORES)))
    out = np.empty((2, 4 * TOK, D), np.float32)
    for k in range(NCORES):
        b, j = k // 4, k % 4
        out[b, j * TOK:(j + 1) * TOK] = res.results[k]["out"]
    return out
```
